# Optimizing a Trainium2 kernel written in Bass

```python
import math
import jax, jax.numpy as jnp
from jax import lax
import numpy as np

D_MODEL = 1024
BATCH = 2
SEQ = 16384
DEPTH = 2

D_SC = 512
SC_KERNEL = 3
D_POOL = 512
POOL_WINDOWS = (2, 4, 8, 16)
N_POOL_GROUPS = 4
POOL_GROUP = D_POOL // N_POOL_GROUPS
POOL_OUT_GROUP = D_MODEL // N_POOL_GROUPS
D_CONF = 512
CONF_KERNEL = 31
N_DIFF_HEADS = 4
DIFF_HEAD_DIM = 64
D_DIFF_QK = N_DIFF_HEADS * 2 * DIFF_HEAD_DIM
D_DIFF_V = N_DIFF_HEADS * 2 * DIFF_HEAD_DIM
Q_BLOCK = 128
N_BRANCHES = 4
N_IN = 3 * D_SC + D_POOL + 2 * D_CONF + 2 * D_DIFF_QK + D_DIFF_V + N_BRANCHES * D_MODEL
D_FF = 2816
FFN_KERNEL = 3
EPS = 1e-6

kernel_name = "hybrid_parallel_gated_mixers"


def rms_norm(x, g):
    xf = x.astype(jnp.float32)
    y = xf * lax.rsqrt(jnp.mean(xf * xf, axis=-1, keepdims=True) + EPS)
    return (y * g.astype(jnp.float32)).astype(x.dtype)


def layer_norm(x, g, b):
    xf = x.astype(jnp.float32)
    mu = jnp.mean(xf, axis=-1, keepdims=True)
    var = jnp.mean(jnp.square(xf - mu), axis=-1, keepdims=True)
    y = (xf - mu) * lax.rsqrt(var + EPS)
    return (y * g.astype(jnp.float32) + b.astype(jnp.float32)).astype(x.dtype)


def causal_dwconv(x, w):
    ksz, ch = w.shape
    rhs = w.astype(x.dtype)[:, None, :]
    return lax.conv_general_dilated(
        x, rhs, window_strides=(1,), padding=[(ksz - 1, 0)],
        dimension_numbers=("NWC", "WIO", "NWC"), feature_group_count=ch)


def pool_mixer(xp, w_pool, scale):
    b, s, _ = xp.shape
    xg = xp.astype(jnp.float32).reshape(b, s, N_POOL_GROUPS, POOL_GROUP)
    cs = jnp.cumsum(xg, axis=1)
    t = jnp.arange(s)
    outs = []
    for g, w in enumerate(POOL_WINDOWS):
        csg = cs[:, :, g]
        lag = jnp.pad(csg, ((0, 0), (w, 0), (0, 0)))[:, :s]
        count = jnp.minimum(t + 1, w).astype(jnp.float32)[None, :, None]
        outs.append((csg - lag) / count - xg[:, :, g])
    pooled = jnp.stack(outs, axis=2).astype(xp.dtype)
    y = jnp.einsum("bsgc,gco->bsgo", pooled, w_pool.astype(xp.dtype))
    return y.reshape(b, s, D_MODEL) * scale.astype(xp.dtype)


def conformer_conv(u, conv_w, conv_b, ln_g, ln_b, w_out):
    a, gate = jnp.split(u, 2, axis=-1)
    y = a * jax.nn.sigmoid(gate)
    y = causal_dwconv(y, conv_w) + conv_b.astype(y.dtype)
    y = layer_norm(y, ln_g, ln_b)
    y = jax.nn.silu(y)
    return y @ w_out.astype(y.dtype)


def diff_attention(q, k, v, q_norm, k_norm, lq1, lk1, lq2, lk2, subln, lam_init):
    b, s, _ = q.shape
    hd = DIFF_HEAD_DIM
    q = rms_norm(q.reshape(b, s, N_DIFF_HEADS, 2, hd), q_norm) * (hd ** -0.5)
    k = rms_norm(k.reshape(b, s, N_DIFF_HEADS, 2, hd), k_norm)
    v = v.reshape(b, s, N_DIFF_HEADS, 2 * hd)
    qt = jnp.transpose(q, (0, 2, 3, 1, 4))
    kt = jnp.transpose(k, (0, 2, 3, 1, 4))
    vt = jnp.transpose(v, (0, 2, 1, 3))
    lam = (jnp.exp(jnp.sum(lq1.astype(jnp.float32) * lk1.astype(jnp.float32)))
           - jnp.exp(jnp.sum(lq2.astype(jnp.float32) * lk2.astype(jnp.float32)))
           + lam_init)
    nb = s // Q_BLOCK
    q_blocks = jnp.moveaxis(qt.reshape(b, N_DIFF_HEADS, 2, nb, Q_BLOCK, hd), 3, 0)
    starts = jnp.arange(nb, dtype=jnp.int32) * Q_BLOCK
    kpos = jnp.arange(s, dtype=jnp.int32)

    def attend(args):
        qb, start = args
        sc = jnp.einsum("bhmqd,bhmkd->bhmqk", qb, kt).astype(jnp.float32)
        qpos = start + jnp.arange(Q_BLOCK, dtype=jnp.int32)
        mask = kpos[None, :] <= qpos[:, None]
        p = jax.nn.softmax(jnp.where(mask, sc, -jnp.inf), axis=-1)
        a = (p[:, :, 0] - lam * p[:, :, 1]).astype(vt.dtype)
        return jnp.einsum("bhqk,bhkv->bhqv", a, vt)

    o = lax.map(attend, (q_blocks, starts))
    o = jnp.transpose(o, (1, 0, 3, 2, 4)).reshape(b, s, N_DIFF_HEADS, 2 * hd)
    o = rms_norm(o, subln) * (1.0 - lam_init)
    return o.reshape(b, s, D_DIFF_V)


def conv_glu_ffn(h, w_up, conv_w, w_down):
    u = causal_dwconv(h @ w_up.astype(h.dtype), conv_w)
    gate, up = jnp.split(u, 2, axis=-1)
    return (jax.nn.silu(gate) * up) @ w_down.astype(h.dtype)


def setup_inputs(seed: int = 0) -> dict:
    key = jax.random.key(seed)
    ks = jax.random.split(key, 32)
    L = DEPTH
    f32 = jnp.float32

    def nrm(k, shape, scale):
        return jax.random.normal(k, shape, f32) * scale

    def gain(k, shape):
        return 1.0 + 0.02 * jax.random.normal(k, shape, f32)

    return {
        "x": nrm(ks[0], (BATCH, SEQ, D_MODEL), 1.0),
        "mix_norm": gain(ks[1], (L, D_MODEL)),
        "w_in": nrm(ks[2], (L, D_MODEL, N_IN), D_MODEL ** -0.5),
        "sc_conv": nrm(ks[3], (L, SC_KERNEL, D_SC), SC_KERNEL ** -0.5),
        "sc_out": nrm(ks[4], (L, D_SC, D_MODEL), D_SC ** -0.5),
        "pool_w": nrm(ks[5], (L, N_POOL_GROUPS, POOL_GROUP, POOL_OUT_GROUP), POOL_GROUP ** -0.5),
        "pool_scale": 1.0 + 0.1 * jax.random.normal(ks[6], (L, D_MODEL), f32),
        "conf_conv": nrm(ks[7], (L, CONF_KERNEL, D_CONF), CONF_KERNEL ** -0.5),
        "conf_conv_b": nrm(ks[8], (L, D_CONF), 0.02),
        "conf_ln_g": gain(ks[9], (L, D_CONF)),
        "conf_ln_b": nrm(ks[10], (L, D_CONF), 0.02),
        "conf_out": nrm(ks[11], (L, D_CONF, D_MODEL), D_CONF ** -0.5),
        "q_norm": gain(ks[12], (L, DIFF_HEAD_DIM)),
        "k_norm": gain(ks[13], (L, DIFF_HEAD_DIM)),
        "lambda_q1": nrm(ks[14], (L, DIFF_HEAD_DIM), 0.1),
        "lambda_k1": nrm(ks[15], (L, DIFF_HEAD_DIM), 0.1),
        "lambda_q2": nrm(ks[16], (L, DIFF_HEAD_DIM), 0.1),
        "lambda_k2": nrm(ks[17], (L, DIFF_HEAD_DIM), 0.1),
        "diff_subln": gain(ks[18], (L, 2 * DIFF_HEAD_DIM)),
        "diff_out": nrm(ks[19], (L, D_DIFF_V, D_MODEL), D_DIFF_V ** -0.5),
        "w_o": nrm(ks[20], (L, D_MODEL, D_MODEL), D_MODEL ** -0.5),
        "ffn_norm": gain(ks[21], (L, D_MODEL)),
        "ffn_up": nrm(ks[22], (L, D_MODEL, 2 * D_FF), D_MODEL ** -0.5),
        "ffn_conv": nrm(ks[23], (L, FFN_KERNEL, 2 * D_FF), FFN_KERNEL ** -0.5),
        "ffn_down": nrm(ks[24], (L, D_FF, D_MODEL), D_FF ** -0.5),
    }


def reference(x, mix_norm, w_in, sc_conv, sc_out, pool_w, pool_scale, conf_conv, conf_conv_b,
              conf_ln_g, conf_ln_b, conf_out, q_norm, k_norm, lambda_q1, lambda_k1, lambda_q2,
              lambda_k2, diff_subln, diff_out, w_o, ffn_norm, ffn_up, ffn_conv, ffn_down):
    b, s, _ = x.shape
    widths = [D_SC, D_SC, D_SC, D_POOL, 2 * D_CONF, D_DIFF_QK, D_DIFF_QK, D_DIFF_V,
              N_BRANCHES * D_MODEL]
    offsets = [int(o) for o in np.cumsum(widths)[:-1]]
    for l in range(DEPTH):
        lam_init = 0.8 - 0.6 * math.exp(-0.3 * l)
        h = rms_norm(x, mix_norm[l])
        z = h @ w_in[l].astype(h.dtype)
        sc_b, sc_c, sc_x, pool_in, conf_in, q, k, v, gate_logits = jnp.split(z, offsets, axis=-1)
        y_sc = (sc_b * causal_dwconv(sc_c * sc_x, sc_conv[l])) @ sc_out[l].astype(h.dtype)
        y_pool = pool_mixer(pool_in, pool_w[l], pool_scale[l])
        y_conf = conformer_conv(conf_in, conf_conv[l], conf_conv_b[l], conf_ln_g[l],
                                conf_ln_b[l], conf_out[l])
        y_diff = diff_attention(q, k, v, q_norm[l], k_norm[l], lambda_q1[l], lambda_k1[l],
                                lambda_q2[l], lambda_k2[l], diff_subln[l], lam_init)
        y_diff = y_diff @ diff_out[l].astype(h.dtype)
        gates = jax.nn.sigmoid(gate_logits.astype(jnp.float32)).reshape(b, s, N_BRANCHES, D_MODEL)
        ys = jnp.stack([y_sc, y_pool, y_conf, y_diff], axis=2)
        merged = jnp.sum(gates.astype(ys.dtype) * ys, axis=2)
        x = x + merged @ w_o[l].astype(h.dtype)
        h2 = rms_norm(x, ffn_norm[l])
        x = x + conv_glu_ffn(h2, ffn_up[l], ffn_conv[l], ffn_down[l])
    return x
```

```python
import contextlib
import math
import types
import numpy as np
import ml_dtypes
import concourse.bass as bass
import concourse.mybir as mybir
from concourse.bass_utils import run_bass_kernel_spmd

F32 = mybir.dt.float32
BF16 = mybir.dt.bfloat16
ALU = mybir.AluOpType
AF = mybir.ActivationFunctionType
NPBF = ml_dtypes.bfloat16

D = 1024
SEQ = 16384
NB = 2
NCORE = 8
TOK = 4096
H = 32
M = 456
W = H + M
NWIN = 9
TPAD = NWIN * M
XW = H + TPAD
N_IN = 8704
D_FF = 2816
EPS = 1e-6
WSLOT = 4096


def _snap(fn):
    if fn is None or fn.__closure__ is None:
        return fn
    cells = []
    for cl in fn.__closure__:
        try:
            cells.append(types.CellType(cl.cell_contents))
        except ValueError:
            cells.append(cl)
    return types.FunctionType(fn.__code__, fn.__globals__, fn.__name__, fn.__defaults__, tuple(cells))


class Ev:
    __slots__ = ("sem", "val")

    def __init__(self, sem, val):
        self.sem = sem
        self.val = val


class Buf:
    def __init__(self, name):
        self.name = name
        self.w = None
        self.r = []


class Eng:
    def __init__(self, name, hn, sem):
        self.name = name
        self.hn = hn
        self.sem = sem
        self.cnt = 0
        self.ops = []
        self.seen = {}


class KB:
    def __init__(self, nc, stack):
        self.nc = nc
        self.stack = stack
        self.engs = {}
        for n, hn in [("pe", "tensor"), ("act", "scalar"), ("dve", "vector"), ("pool", "gpsimd"), ("sp", "sync")]:
            sem = stack.enter_context(nc.semaphore("s_" + n))
            self.engs[n] = Eng(n, hn, sem)
        self.dma_sems = {}
        self.nbuf = 0
        self.pools = {}

    def sb(self, name, shape, dt):
        t = self.stack.enter_context(self.nc.sbuf_tensor("sb_" + name, shape, dt))
        return t, Buf(name)

    def ps(self, name, shape, dt=F32):
        t = self.stack.enter_context(self.nc.psum_tensor("ps_" + name, shape, dt))
        return t, Buf(name)

    def pool(self, name, n, shape, dt, psum=False):
        items = [(self.ps if psum else self.sb)(f"{name}{i}", shape, dt) for i in range(n)]
        self.pools[name] = [items, 0]

    def get(self, name):
        p = self.pools[name]
        it = p[0][p[1] % len(p[0])]
        p[1] += 1
        return it

    def _deps(self, eng, reads, writes):
        evs = []
        for b in reads:
            if b.w is not None:
                evs.append(b.w)
        for b in writes:
            if b.w is not None:
                evs.append(b.w)
            evs.extend(b.r)
        best = {}
        for e in evs:
            k = id(e.sem)
            if k not in best or best[k].val < e.val:
                best[k] = e
        waits = []
        for k, e in best.items():
            if eng.name == "pe" and e.sem is eng.sem:
                continue
            if eng.seen.get(k, 0) >= e.val:
                continue
            eng.seen[k] = e.val
            waits.append((e.sem, e.val))
        return waits

    def _post(self, ev, reads, writes):
        for b in reads:
            b.r.append(ev)
            if len(b.r) > 64:
                best = {}
                for e in b.r:
                    k = id(e.sem)
                    if k not in best or best[k].val < e.val:
                        best[k] = e
                b.r = list(best.values())
        for b in writes:
            b.w = ev
            b.r = []

    def op(self, engn, fn, reads=(), writes=()):
        eng = self.engs[engn]
        waits = self._deps(eng, reads, writes)
        eng.cnt += 1
        ev = Ev(eng.sem, eng.cnt)
        eng.ops.append((waits, _snap(fn), eng.sem, 1))
        self._post(ev, reads, writes)
        return ev

    def dma(self, engn, key, fn, reads=(), writes=()):
        eng = self.engs[engn]
        if key not in self.dma_sems:
            sem = self.stack.enter_context(self.nc.semaphore("d_" + str(key)))
            self.dma_sems[key] = [sem, 0]
        rec = self.dma_sems[key]
        waits = self._deps(eng, reads, writes)
        rec[1] += 16
        ev = Ev(rec[0], rec[1])
        eng.ops.append((waits, _snap(fn), rec[0], 16))
        self._post(ev, reads, writes)
        return ev

    def wait_all(self, engn, bufs):
        eng = self.engs[engn]
        waits = self._deps(eng, bufs, [])
        eng.ops.append((waits, None, None, 0))

    def emit(self):
        with self.nc.Block() as block:
            for n, eng in self.engs.items():
                if not eng.ops:
                    continue
                dec = getattr(block, eng.hn)

                def body(e, eng=eng):
                    for waits, fn, sem, inc in eng.ops:
                        for (s, v) in waits:
                            e.wait_ge(s, v)
                        if fn is not None:
                            fn(e).then_inc(sem, inc)
                dec(body)


class Ctx:
    pass


def setup_common(kb, c):
    nc = kb.nc
    c.ones_t, c.ones_b = kb.sb("ones", [128, 128], BF16)
    c.eps_t, c.eps_b = kb.sb("eps", [128, 1], F32)
    kb.op("pool", lambda e: e.memset(c.ones_t[:], 1.0 / 1024.0), writes=[c.ones_b])
    kb.op("pool", lambda e: e.memset(c.eps_t[:], EPS), writes=[c.eps_b])
    kb.pool("ps", 8, [128, 512], F32, psum=True)
    kb.pool("wst", 3, [128, 2048], F32)
    kb.pool("wbf", 3, [128, WSLOT], BF16)
    c.wslot_i = 0


def wload(kb, c, views, KC, NCOL):
    wb, wbb = kb.get("wbf")
    kstep = max(1, 2048 // NCOL)
    for k0 in range(0, KC, kstep):
        k1 = min(KC, k0 + kstep)
        st, stb = kb.get("wst")
        key = "wst%d" % (c.wslot_i % 3)
        c.wslot_i += 1
        nel = (k1 - k0) * NCOL
        for (v, off, ncol) in views:
            dst = st[:, 0:nel].rearrange("p (k n) -> p k n", k=k1 - k0)[:, :, off:off + ncol]
            kb.dma("sp", key, lambda e, dst=dst, v=v, k0=k0, k1=k1: e.dma_start(out=dst, in_=v[:, k0:k1, :]), writes=[stb])
        kb.op("pool", lambda e, st=st, k0=k0, nel=nel: e.tensor_copy(out=wb[:, k0 * NCOL:k0 * NCOL + nel], in_=st[:, 0:nel]),
              reads=[stb], writes=[wbb])
    return wb, wbb


def rmsnorm_h(kb, c, xs, xsb, g_t, g_b, h_t, h_b, sq_t, sq_b):
    kb.op("act", lambda e: e.activation(out=sq_t[:, :], in_=xs[:, :], func=AF.Square), reads=[xsb], writes=[sq_b])
    pt, pb = kb.get("ps")
    for kc in range(8):
        kb.op("pe", lambda e, kc=kc: e.matmul(pt[:, 0:W], lhsT=c.ones_t[:, :], rhs=sq_t[:, kc * W:(kc + 1) * W],
                                              start=(kc == 0), stop=(kc == 7)), reads=[sq_b, c.ones_b], writes=[pb])
    rt, rb = kb.get("t32")
    kb.op("act", lambda e: e.activation(out=rt[:, 0:W], in_=pt[:, 0:W], func=AF.Ln, bias=c.eps_t[:, 0:1]),
          reads=[pb, c.eps_b], writes=[rb])
    kb.op("act", lambda e: e.activation(out=rt[:, 0:W], in_=rt[:, 0:W], func=AF.Exp, scale=-0.5), reads=[rb], writes=[rb])
    for kc in range(8):
        kb.op("dve", lambda e, kc=kc: e.scalar_tensor_tensor(out=h_t[:, kc * W:(kc + 1) * W], in0=xs[:, kc * W:(kc + 1) * W],
                                                             scalar=g_t[:, kc:kc + 1], in1=rt[:, 0:W], op0=ALU.mult, op1=ALU.mult),
              reads=[xsb, g_b, rb], writes=[h_b])


def proj(kb, wt, wbuf, KC, NCOL, col, rhs_t, rhs_b, rstride, roff, n, extra_reads=()):
    pt, pb = kb.get("ps")
    for kc in range(KC):
        kb.op("pe", lambda e, kc=kc: e.matmul(pt[:, 0:n], lhsT=wt[:, kc * NCOL + col:kc * NCOL + col + 128],
                                              rhs=rhs_t[:, kc * rstride + roff:kc * rstride + roff + n],
                                              start=(kc == 0), stop=(kc == KC - 1)),
              reads=[wbuf, rhs_b, *extra_reads], writes=[pb])
    return pt, pb


def load_cols(kb, c, dram, nvec, name):
    t, b = kb.sb(name, [128, nvec], F32)
    kb.dma("sp", name, lambda e: e.dma_start(out=t[:, :], in_=dram.rearrange("(k p) -> p k", p=128),
                                             allow_slow_non_contiguous=True), writes=[b])
    return t, b


def build_l1(nwin=NWIN, stage=99):
    nc = bass.Bass("TRN2", target_bir_lowering=False)
    dt = nc.dram_tensor
    xT = dt("xT", [D, XW], F32, kind="ExternalInput").ap()
    w_in = dt("w_in", [D, N_IN], F32, kind="ExternalInput").ap()
    mixn = dt("mix_norm", [D], F32, kind="ExternalInput").ap()
    sc_conv = dt("sc_conv", [3, 512], F32, kind="ExternalInput").ap()
    sc_out = dt("sc_out", [512, D], F32, kind="ExternalInput").ap()
    pool_w = dt("pool_w", [4, 128, 256], F32, kind="ExternalInput").ap()
    pool_scale = dt("pool_scale", [D], F32, kind="ExternalInput").ap()
    pool_corr = dt("pool_corr", [4, 16], F32, kind="ExternalInput").ap()
    conf_conv = dt("conf_conv", [31, 512], F32, kind="ExternalInput").ap()
    conf_b = dt("conf_conv_b", [512], F32, kind="ExternalInput").ap()
    ln_g = dt("conf_ln_g", [512], F32, kind="ExternalInput").ap()
    ln_b = dt("conf_ln_b", [512], F32, kind="ExternalInput").ap()
    conf_out = dt("conf_out", [512, D], F32, kind="ExternalInput").ap()
    q_norm = dt("q_norm", [64], F32, kind="ExternalInput").ap()
    k_norm = dt("k_norm", [64], F32, kind="ExternalInput").ap()
    ident_in = dt("ident", [128, 128], F32, kind="ExternalInput").ap()
    blk_in = dt("blk64", [128, 128], F32, kind="ExternalInput").ap()
    qkv_o = dt("qkv", [1536, TPAD], BF16, kind="ExternalOutput").ap()
    gd_o = dt("gd", [D, TPAD], BF16, kind="ExternalOutput").ap()
    part_o = dt("part", [D, TPAD], F32, kind="ExternalOutput").ap()

    with contextlib.ExitStack() as st:
        kb = KB(nc, st)
        c = Ctx()
        setup_common(kb, c)
        kb.pool("t32", 7, [128, W], F32)
        kb.pool("tb", 2, [128, W], BF16)
        kb.pool("xs", 1, [128, 8 * W], F32)
        sq_t, sq_b = kb.sb("sq", [128, 8 * W], BF16)
        h_t, h_b = kb.sb("h", [128, 8 * W], BF16)
        ain_t, ain_b = kb.sb("ain", [128, 4 * M], BF16)
        pl_t, pl_b = kb.sb("pl", [128, 4 * M], BF16)
        glu_t, glu_b = kb.sb("glu", [128, 4 * W], BF16)
        yc_t, yc_b = kb.sb("yc", [128, 4 * M], F32)
        cx_t, cx_b = kb.sb("cx", [128, 4 * W], F32)
        ysq_t, ysq_b = cx_t, cx_b
        kb.pool("mg", 1, [128, 8 * M], F32)
        kb.pool("qkvs", 1, [128, 12 * M], BF16)
        kb.pool("gds", 1, [128, 8 * M], BF16)
        g_t, g_b = load_cols(kb, c, mixn, 8, "g_mix")
        pscale_t, pscale_b = load_cols(kb, c, pool_scale, 8, "pscale")
        cb_t, cb_b = load_cols(kb, c, conf_b, 4, "cb")
        lg_t, lg_b = load_cols(kb, c, ln_g, 4, "lg")
        lb_t, lb_b = load_cols(kb, c, ln_b, 4, "lb")
        scw_t, scw_b = kb.sb("scw", [128, 3 * 4], F32)
        kb.dma("sp", "scw", lambda e: e.dma_start(out=scw_t[:, :].rearrange("p (t k) -> p t k", t=3),
                                                  in_=sc_conv.rearrange("t (k p) -> p t k", p=128),
                                                  allow_slow_non_contiguous=True), writes=[scw_b])
        cfw_t, cfw_b = kb.sb("cfw", [128, 31 * 4], F32)
        kb.dma("sp", "cfw", lambda e: e.dma_start(out=cfw_t[:, :].rearrange("p (t k) -> p t k", t=31),
                                                  in_=conf_conv.rearrange("t (k p) -> p t k", p=128),
                                                  allow_slow_non_contiguous=True), writes=[cfw_b])
        corr_t, corr_b = kb.sb("corr", [128, 64], F32)
        kb.dma("sp", "corr", lambda e: e.dma_start(out=corr_t[:, :], in_=pool_corr.rearrange("g t -> (g t)").partition_broadcast(128)),
               writes=[corr_b])
        qn_t, qn_b = kb.sb("qn", [128, 2], F32)
        for half in range(2):
            kb.dma("sp", "qn", lambda e, half=half: e.dma_start(out=qn_t[64 * half:64 * half + 64, 0:1],
                                                                in_=q_norm.rearrange("(p o) -> p o", o=1),
                                                                allow_slow_non_contiguous=True), writes=[qn_b])
            kb.dma("sp", "qn", lambda e, half=half: e.dma_start(out=qn_t[64 * half:64 * half + 64, 1:2],
                                                                in_=k_norm.rearrange("(p o) -> p o", o=1),
                                                                allow_slow_non_contiguous=True), writes=[qn_b])
        kb.op("dve", lambda e: e.tensor_scalar(out=qn_t[:, 0:1], in0=qn_t[:, 0:1], scalar1=0.125, scalar2=None, op0=ALU.mult),
              reads=[qn_b], writes=[qn_b])
        id_t, id_b = kb.sb("ident", [128, 128], F32)
        kb.dma("sp", "ident", lambda e: e.dma_start(out=id_t[:, :], in_=ident_in), writes=[id_b])
        blkf_t, blkf_b = kb.sb("blkf", [128, 128], F32)
        kb.dma("sp", "blkf", lambda e: e.dma_start(out=blkf_t[:, :], in_=blk_in), writes=[blkf_b])
        blk_t, blk_b = kb.sb("blk", [128, 128], BF16)
        kb.op("dve", lambda e: e.tensor_scalar(out=blk_t[:, :], in0=blkf_t[:, :], scalar1=1.0 / 64.0, scalar2=None, op0=ALU.mult),
              reads=[blkf_b], writes=[blk_b])
        o512_t, o512_b = kb.sb("o512", [128, 128], BF16)
        kb.op("pool", lambda e: e.memset(o512_t[:], 1.0 / 512.0), writes=[o512_b])
        dg_t, dg_b = kb.sb("dg", [128, 124 * 128], BF16)
        for i in range(124):
            kb.op("pool" if i % 2 else "dve",
                  lambda e, i=i: e.tensor_scalar(out=dg_t[:, i * 128:(i + 1) * 128], in0=id_t[:, :], scalar1=cfw_t[:, i:i + 1],
                                                 scalar2=None, op0=ALU.mult),
                  reads=[id_b, cfw_b], writes=[dg_b])

        win_v = w_in.rearrange("(k p) n -> p k n", p=128)

        def wgroup(col0):
            return wload(kb, c, [(win_v[:, :, col0:col0 + 512], 0, 512)], 8, 512)

        def gated_acc(first, mg_t, mg_b, oc, y_pt, y_pb, gt, gb, scale_ap=None, scale_b=None):
            dst = mg_t[:, oc * M:(oc + 1) * M]
            if first:
                kb.op("dve", lambda e: e.tensor_tensor(out=dst, in0=y_pt[:, 0:M], in1=gt[:, 0:M], op=ALU.mult),
                      reads=[y_pb, gb], writes=[mg_b])
            else:
                tt, tb_ = kb.get("t32")
                if scale_ap is None:
                    kb.op("dve", lambda e: e.tensor_tensor(out=tt[:, 0:M], in0=y_pt[:, 0:M], in1=gt[:, 0:M], op=ALU.mult),
                          reads=[y_pb, gb], writes=[tb_])
                else:
                    kb.op("dve", lambda e: e.scalar_tensor_tensor(out=tt[:, 0:M], in0=y_pt[:, 0:M], scalar=scale_ap, in1=gt[:, 0:M],
                                                                  op0=ALU.mult, op1=ALU.mult),
                          reads=[y_pb, gb, scale_b], writes=[tb_])
                kb.op("pool", lambda e: e.tensor_tensor(out=dst, in0=dst, in1=tt[:, 0:M], op=ALU.add),
                      reads=[tb_, mg_b], writes=[mg_b])

        def branch_out(first, mg_t, mg_b, gate_col0, ywt, ywb, ykc, yncol, rhs_t, rhs_b, ycolfn, scale_fn=None):
            gw = None
            for oc in range(8):
                if oc % 4 == 0:
                    gw = wgroup(gate_col0 + 512 * (oc // 4))
                gpt, gpb = proj(kb, gw[0], gw[1], 8, 512, (oc % 4) * 128, h_t, h_b, W, H, M)
                gt, gb = kb.get("t32")
                kb.op("act", lambda e, gt=gt, gpt=gpt: e.activation(out=gt[:, 0:M], in_=gpt[:, 0:M], func=AF.Sigmoid),
                      reads=[gpb], writes=[gb])
                wt_, wb_, col = ycolfn(oc)
                ypt, ypb = proj(kb, wt_, wb_, ykc, yncol, col, rhs_t, rhs_b, M, 0, M)
                if scale_fn is None:
                    gated_acc(first, mg_t, mg_b, oc, ypt, ypb, gt, gb)
                else:
                    gated_acc(first, mg_t, mg_b, oc, ypt, ypb, gt, gb, scale_fn(oc), pscale_b)

        for wi in range(nwin):
            col0 = wi * M
            xs, xsb = kb.get("xs")
            kb.dma("sp", "xs0",
                   lambda e, xs=xs, col0=col0: e.dma_start(out=xs[:, :].rearrange("p (k n) -> p k n", k=8),
                                                           in_=xT.rearrange("(k p) n -> p k n", p=128)[:, :, col0:col0 + W]),
                   writes=[xsb])
            rmsnorm_h(kb, c, xs, xsb, g_t, g_b, h_t, h_b, sq_t, sq_b)
            mg_t, mg_b = kb.get("mg")
            qkvs_t, qkvs_b = kb.get("qkvs")
            gds_t, gds_b = kb.get("gds")

            if stage < 1:
                continue
            wc = wgroup(512)
            for m in range(4):
                cpt, cpb = proj(kb, wc[0], wc[1], 8, 512, m * 128, h_t, h_b, W, 0, W)
                kb.op("act", lambda e, cpt=cpt, m=m: e.activation(out=cx_t[:, m * W:(m + 1) * W], in_=cpt[:, 0:W], func=AF.Copy),
                      reads=[cpb], writes=[cx_b])
            wx = wgroup(1024)
            for m in range(4):
                xpt, xpb = proj(kb, wx[0], wx[1], 8, 512, m * 128, h_t, h_b, W, 0, W)
                ct, cb_ = kb.get("t32")
                kb.op("dve", lambda e, ct=ct, xpt=xpt, m=m: e.tensor_tensor(out=ct[:, 0:W], in0=xpt[:, 0:W], in1=cx_t[:, m * W:(m + 1) * W], op=ALU.mult),
                      reads=[xpb, cx_b], writes=[cb_])
                yv = yc_t[:, m * M:(m + 1) * M]
                kb.op("dve", lambda e, yv=yv, ct=ct, m=m: e.tensor_scalar(out=yv, in0=ct[:, H - 2:W - 2], scalar1=scw_t[:, m:m + 1],
                                                                         scalar2=None, op0=ALU.mult),
                      reads=[cb_, scw_b], writes=[yc_b])
                for tap in (1, 2):
                    kb.op("dve", lambda e, yv=yv, ct=ct, m=m, tap=tap: e.scalar_tensor_tensor(
                        out=yv, in0=ct[:, H - 2 + tap:W - 2 + tap], scalar=scw_t[:, tap * 4 + m:tap * 4 + m + 1],
                        in1=yv, op0=ALU.mult, op1=ALU.add), reads=[cb_, scw_b, yc_b], writes=[yc_b])
            wbg = wgroup(0)
            for m in range(4):
                bpt, bpb = proj(kb, wbg[0], wbg[1], 8, 512, m * 128, h_t, h_b, W, H, M)
                kb.op("dve", lambda e, bpt=bpt, m=m: e.tensor_tensor(out=ain_t[:, m * M:(m + 1) * M], in0=bpt[:, 0:M], in1=yc_t[:, m * M:(m + 1) * M],
                                                                    op=ALU.mult), reads=[bpb, yc_b], writes=[ain_b])
            wa = wload(kb, c, [(sc_out.rearrange("(k p) n -> p k n", p=128), 0, 1024)], 4, 1024)
            branch_out(True, mg_t, mg_b, 4608, None, None, 4, 1024, ain_t, ain_b, lambda oc: (wa[0], wa[1], oc * 128))

            if stage < 2:
                continue
            wp = wgroup(1536)
            for g in range(4):
                ppt, ppb = proj(kb, wp[0], wp[1], 8, 512, g * 128, h_t, h_b, W, 0, W)
                p_t, p_b = kb.get("t32")
                kb.op("act", lambda e, p_t=p_t, ppt=ppt: e.activation(out=p_t[:, 0:W], in_=ppt[:, 0:W], func=AF.Copy),
                      reads=[ppb], writes=[p_b])
                cur_t, cur_b = p_t, p_b
                step = 1
                wlen = 2 ** (g + 1)
                while step < wlen:
                    nt, nb_ = kb.get("t32")
                    kb.op("pool", lambda e, nt=nt, cur_t=cur_t, step=step: e.tensor_tensor(
                        out=nt[:, 16:W], in0=cur_t[:, 16:W], in1=cur_t[:, 16 - step:W - step], op=ALU.add),
                        reads=[cur_b], writes=[nb_])
                    cur_t, cur_b = nt, nb_
                    step *= 2
                dstp = pl_t[:, g * M:(g + 1) * M]
                kb.op("dve", lambda e, cur_t=cur_t, p_t=p_t, dstp=dstp, wlen=wlen: e.scalar_tensor_tensor(
                    out=dstp, in0=cur_t[:, H:W], scalar=1.0 / wlen, in1=p_t[:, H:W], op0=ALU.mult, op1=ALU.subtract),
                    reads=[cur_b, p_b], writes=[pl_b])
                if wi == 0:
                    ft, fb = kb.get("t32")
                    kb.op("dve", lambda e, ft=ft, cur_t=cur_t, g=g: e.tensor_tensor(out=ft[:, 0:16], in0=cur_t[:, H:H + 16],
                                                                                  in1=corr_t[:, g * 16:(g + 1) * 16], op=ALU.mult),
                          reads=[cur_b, corr_b], writes=[fb])
                    kb.op("dve", lambda e, ft=ft, p_t=p_t, g=g: e.tensor_tensor(out=pl_t[:, g * M:g * M + 16], in0=ft[:, 0:16],
                                                                              in1=p_t[:, H:H + 16], op=ALU.subtract),
                          reads=[fb, p_b, pl_b], writes=[pl_b])
            wpw = wload(kb, c, [(pool_w.rearrange("g p n -> p g n"), 0, 256)], 4, 256)

            def ycol_pool(oc):
                return None

            gw = None
            for oc in range(8):
                if oc % 4 == 0:
                    gw = wgroup(4608 + 1024 + 512 * (oc // 4))
                gpt, gpb = proj(kb, gw[0], gw[1], 8, 512, (oc % 4) * 128, h_t, h_b, W, H, M)
                gt, gb = kb.get("t32")
                kb.op("act", lambda e, gt=gt, gpt=gpt: e.activation(out=gt[:, 0:M], in_=gpt[:, 0:M], func=AF.Sigmoid),
                      reads=[gpb], writes=[gb])
                g = oc // 2
                ypt, ypb = kb.get("ps")
                kb.op("pe", lambda e, ypt=ypt, g=g, oc=oc, wpw=wpw: e.matmul(ypt[:, 0:M], lhsT=wpw[0][:, g * 256 + (oc % 2) * 128:g * 256 + (oc % 2) * 128 + 128],
                                                                  rhs=pl_t[:, g * M:(g + 1) * M], start=True, stop=True),
                      reads=[wpw[1], pl_b], writes=[ypb])
                gated_acc(False, mg_t, mg_b, oc, ypt, ypb, gt, gb, pscale_t[:, oc:oc + 1], pscale_b)

            if stage < 3:
                continue
            wcg = wgroup(2560)
            for m in range(4):
                gpt, gpb = proj(kb, wcg[0], wcg[1], 8, 512, m * 128, h_t, h_b, W, 0, W)
                kb.op("act", lambda e, gpt=gpt, m=m: e.activation(out=cx_t[:, m * W:(m + 1) * W], in_=gpt[:, 0:W], func=AF.Sigmoid),
                      reads=[gpb], writes=[cx_b])
            wca = wgroup(2048)
            for m in range(4):
                apt, apb = proj(kb, wca[0], wca[1], 8, 512, m * 128, h_t, h_b, W, 0, W)
                kb.op("dve", lambda e, apt=apt, m=m: e.tensor_tensor(out=glu_t[:, m * W:(m + 1) * W], in0=apt[:, 0:W], in1=cx_t[:, m * W:(m + 1) * W],
                                                                    op=ALU.mult), reads=[apb, cx_b], writes=[glu_b])
            for m in range(4):
                ypt, ypb = kb.get("ps")
                for tap in range(31):
                    i = tap * 4 + m
                    kb.op("pe", lambda e, ypt=ypt, i=i, m=m, tap=tap: e.matmul(
                        ypt[:, 0:M], lhsT=dg_t[:, i * 128:(i + 1) * 128], rhs=glu_t[:, m * W + H - 30 + tap:m * W + H - 30 + tap + M],
                        start=(tap == 0), stop=(tap == 30)), reads=[dg_b, glu_b], writes=[ypb])
                kb.op("act", lambda e, ypt=ypt, m=m: e.activation(out=yc_t[:, m * M:(m + 1) * M], in_=ypt[:, 0:M], func=AF.Identity,
                                                                 bias=cb_t[:, m:m + 1]), reads=[ypb, cb_b], writes=[yc_b])
            kb.op("dve", lambda e: e.tensor_copy(out=pl_t[:, :], in_=yc_t[:, :]), reads=[yc_b], writes=[pl_b])
            kb.op("act", lambda e: e.activation(out=ain_t[:, :], in_=yc_t[:, :], func=AF.Square), reads=[yc_b], writes=[ain_b])
            mpt, mpb = kb.get("ps")
            qpt, qpb = kb.get("ps")
            for m in range(4):
                kb.op("pe", lambda e, m=m, mpt=mpt: e.matmul(mpt[:, 0:M], lhsT=o512_t[:, :], rhs=pl_t[:, m * M:(m + 1) * M], start=(m == 0), stop=(m == 3)),
                      reads=[o512_b, pl_b], writes=[mpb])
            for m in range(4):
                kb.op("pe", lambda e, m=m, qpt=qpt: e.matmul(qpt[:, 0:M], lhsT=o512_t[:, :], rhs=ain_t[:, m * M:(m + 1) * M], start=(m == 0), stop=(m == 3)),
                      reads=[o512_b, ain_b], writes=[qpb])
            mean_t, mean_b = kb.get("t32")
            kb.op("act", lambda e: e.activation(out=mean_t[:, 0:M], in_=mpt[:, 0:M], func=AF.Copy), reads=[mpb], writes=[mean_b])
            var_t, var_b = kb.get("t32")
            kb.op("dve", lambda e: e.tensor_tensor(out=var_t[:, 0:M], in0=mean_t[:, 0:M], in1=mean_t[:, 0:M], op=ALU.mult),
                  reads=[mean_b], writes=[var_b])
            kb.op("dve", lambda e: e.tensor_tensor(out=var_t[:, 0:M], in0=qpt[:, 0:M], in1=var_t[:, 0:M], op=ALU.subtract),
                  reads=[qpb, var_b], writes=[var_b])
            kb.op("act", lambda e: e.activation(out=var_t[:, 0:M], in_=var_t[:, 0:M], func=AF.Ln, bias=c.eps_t[:, 0:1]),
                  reads=[var_b, c.eps_b], writes=[var_b])
            kb.op("act", lambda e: e.activation(out=var_t[:, 0:M], in_=var_t[:, 0:M], func=AF.Exp, scale=-0.5), reads=[var_b], writes=[var_b])
            for m in range(4):
                tt, tb_ = kb.get("t32")
                kb.op("dve", lambda e, tt=tt, m=m: e.tensor_tensor(out=tt[:, 0:M], in0=yc_t[:, m * M:(m + 1) * M], in1=mean_t[:, 0:M], op=ALU.subtract),
                      reads=[yc_b, mean_b], writes=[tb_])
                kb.op("dve", lambda e, tt=tt, m=m: e.scalar_tensor_tensor(out=tt[:, 0:M], in0=tt[:, 0:M], scalar=lg_t[:, m:m + 1], in1=var_t[:, 0:M],
                                                                         op0=ALU.mult, op1=ALU.mult), reads=[tb_, lg_b, var_b], writes=[tb_])
                kb.op("act", lambda e, tt=tt, m=m: e.activation(out=ain_t[:, m * M:(m + 1) * M], in_=tt[:, 0:M], func=AF.Silu, bias=lb_t[:, m:m + 1]),
                      reads=[tb_, lb_b], writes=[ain_b])
            wco = wload(kb, c, [(conf_out.rearrange("(k p) n -> p k n", p=128), 0, 1024)], 4, 1024)
            branch_out(False, mg_t, mg_b, 4608 + 2048, None, None, 4, 1024, ain_t, ain_b, lambda oc: (wco[0], wco[1], oc * 128))

            if stage < 4:
                continue
            for which in range(3):
                wq = wgroup(3072 + 512 * which)
                for m in range(4):
                    ppt, ppb = proj(kb, wq[0], wq[1], 8, 512, m * 128, h_t, h_b, W, H, M)
                    dst = qkvs_t[:, (which * 4 + m) * M:(which * 4 + m + 1) * M]
                    if which == 2:
                        kb.op("act", lambda e, dst=dst, ppt=ppt: e.activation(out=dst, in_=ppt[:, 0:M], func=AF.Copy), reads=[ppb], writes=[qkvs_b])
                        continue
                    qs_t, qs_b = kb.get("t32")
                    kb.op("act", lambda e, qs_t=qs_t, ppt=ppt: e.activation(out=qs_t[:, 0:M], in_=ppt[:, 0:M], func=AF.Copy), reads=[ppb], writes=[qs_b])
                    s_t, s_b = kb.get("tb")
                    kb.op("act", lambda e, s_t=s_t, ppt=ppt: e.activation(out=s_t[:, 0:M], in_=ppt[:, 0:M], func=AF.Square), reads=[ppb], writes=[s_b])
                    spt, spb = kb.get("ps")
                    kb.op("pe", lambda e, spt=spt, s_t=s_t: e.matmul(spt[:, 0:M], lhsT=blk_t[:, :], rhs=s_t[:, 0:M], start=True, stop=True),
                          reads=[blk_b, s_b], writes=[spb])
                    r_t, r_b = kb.get("t32")
                    kb.op("act", lambda e, r_t=r_t, spt=spt: e.activation(out=r_t[:, 0:M], in_=spt[:, 0:M], func=AF.Ln, bias=c.eps_t[:, 0:1]),
                          reads=[spb, c.eps_b], writes=[r_b])
                    kb.op("act", lambda e, r_t=r_t: e.activation(out=r_t[:, 0:M], in_=r_t[:, 0:M], func=AF.Exp, scale=-0.5), reads=[r_b], writes=[r_b])
                    kb.op("dve", lambda e, dst=dst, qs_t=qs_t, r_t=r_t, which=which: e.scalar_tensor_tensor(
                        out=dst, in0=qs_t[:, 0:M], scalar=qn_t[:, which:which + 1], in1=r_t[:, 0:M], op0=ALU.mult, op1=ALU.mult),
                        reads=[qs_b, qn_b, r_b], writes=[qkvs_b])
            gw = None
            for oc in range(8):
                if oc % 4 == 0:
                    gw = wgroup(4608 + 3072 + 512 * (oc // 4))
                gpt, gpb = proj(kb, gw[0], gw[1], 8, 512, (oc % 4) * 128, h_t, h_b, W, H, M)
                kb.op("act", lambda e, gpt=gpt, oc=oc, gds_t=gds_t: e.activation(out=gds_t[:, oc * M:(oc + 1) * M], in_=gpt[:, 0:M], func=AF.Sigmoid),
                      reads=[gpb], writes=[gds_b])

            ob = Buf("out")
            for part3 in range(3):
                kb.dma("sp", "qkvs0", lambda e, qkvs_t=qkvs_t, col0=col0, part3=part3: e.dma_start(
                    out=qkv_o[512 * part3:512 * part3 + 512, :].rearrange("(k p) n -> p k n", p=128)[:, :, col0:col0 + M],
                    in_=qkvs_t[:, part3 * 4 * M:(part3 + 1) * 4 * M].rearrange("p (k n) -> p k n", k=4)),
                    reads=[qkvs_b], writes=[ob])
            kb.dma("sp", "gds0", lambda e, gds_t=gds_t, col0=col0: e.dma_start(
                out=gd_o.rearrange("(k p) n -> p k n", p=128)[:, :, col0:col0 + M], in_=gds_t[:, :].rearrange("p (k n) -> p k n", k=8)),
                reads=[gds_b], writes=[ob])
            kb.dma("sp", "mg0", lambda e, mg_t=mg_t, col0=col0: e.dma_start(
                out=part_o.rearrange("(k p) n -> p k n", p=128)[:, :, col0:col0 + M], in_=mg_t[:, :].rearrange("p (k n) -> p k n", k=8)),
                reads=[mg_b], writes=[ob])
            c.last_out = [qkvs_b, gds_b, mg_b]
            if wi >= nwin - 1:
                c.__dict__.setdefault("finals", []).extend([qkvs_b, gds_b, mg_b])
        eng = kb.engs["sp"]
        waits = kb._deps(eng, [], getattr(c, 'finals', [h_b]))
        eng.ops.append((waits, None, None, 0))
        kb.emit()
    return nc


def halo_slice(arrT, c):
    C = arrT.shape[0]
    out = np.zeros((C, XW), dtype=arrT.dtype)
    lo = c * TOK - H
    hi = c * TOK + TPAD
    slo, shi = max(lo, 0), min(hi, SEQ)
    out[:, slo - lo:shi - lo] = arrT[:, slo:shi]
    return out


def consts():
    ident = np.eye(128, dtype=np.float32)
    blk = np.zeros((128, 128), np.float32)
    blk[:64, :64] = 1.0
    blk[64:, 64:] = 1.0
    return ident, blk


def pool_corr_for(c):
    corr = np.zeros((4, 16), np.float32)
    for g, w in enumerate((2, 4, 8, 16)):
        for t in range(16):
            corr[g, t] = 1.0 / (min(t + 1, w) if c == 0 else w)
    return corr


def l1_maps(inp, l, xT_b):
    ident, blk = consts()
    maps = []
    for core in range(NCORE):
        b, c = divmod(core, 4)
        m = {"xT": halo_slice(xT_b[b], c), "ident": ident, "blk64": blk, "pool_corr": pool_corr_for(c)}
        for k in ("w_in", "mix_norm", "sc_conv", "sc_out", "pool_w", "pool_scale", "conf_conv", "conf_conv_b",
                  "conf_ln_g", "conf_ln_b", "conf_out", "q_norm", "k_norm"):
            m[k] = np.ascontiguousarray(inp[k][l])
        maps.append(m)
    return maps


def build_l3(nwin=NWIN):
    nc = bass.Bass("TRN2", target_bir_lowering=False)
    dt = nc.dram_tensor
    xT = dt("xT", [D, XW], F32, kind="ExternalInput").ap()
    partT = dt("partT", [D, XW], F32, kind="ExternalInput").ap()
    gdT = dt("gdT", [D, XW], BF16, kind="ExternalInput").ap()
    oT = dt("oT", [512, XW], BF16, kind="ExternalInput").ap()
    diff_out = dt("diff_out", [512, D], F32, kind="ExternalInput").ap()
    w_o = dt("w_o", [D, D], F32, kind="ExternalInput").ap()
    ffn_norm = dt("ffn_norm", [D], F32, kind="ExternalInput").ap()
    ffn_up = dt("ffn_up", [D, 2 * D_FF], F32, kind="ExternalInput").ap()
    ffn_conv = dt("ffn_conv", [3, 2 * D_FF], F32, kind="ExternalInput").ap()
    ffn_down = dt("ffn_down", [D_FF, D], F32, kind="ExternalInput").ap()
    xo = dt("xo", [D, TPAD], F32, kind="ExternalOutput").ap()

    with contextlib.ExitStack() as st:
        kb = KB(nc, st)
        c = Ctx()
        setup_common(kb, c)
        kb.pool("t32", 6, [128, W], F32)
        xs, xsb = kb.sb("xs", [128, 8 * W], F32)
        pa_t, pa_b = kb.sb("pa", [128, 8 * W], F32)
        gd_t, gd_b = kb.sb("gd", [128, 8 * W], BF16)
        o_t, o_b = kb.sb("o", [128, 4 * W], BF16)
        m_t, m_b = kb.sb("m", [128, 8 * W], BF16)
        xm_t, xm_b = kb.sb("xm", [128, 8 * W], F32)
        sq_t, sq_b = kb.sb("sq", [128, 8 * W], BF16)
        h_t, h_b = kb.sb("h", [128, 8 * W], BF16)
        f_t, f_b = kb.sb("f", [128, 22 * M], BF16)
        xo_t, xo_b = kb.sb("xo", [128, 8 * M], F32)
        g_t, g_b = load_cols(kb, c, ffn_norm, 8, "g_ffn")
        fw_t, fw_b = kb.sb("fw", [128, 3 * 44], F32)
        kb.dma("sp", "fw", lambda e: e.dma_start(out=fw_t[:, :].rearrange("p (t k) -> p t k", t=3),
                                                 in_=ffn_conv.rearrange("t (k p) -> p t k", p=128),
                                                 allow_slow_non_contiguous=True), writes=[fw_b])
        up_v = ffn_up.rearrange("(k p) n -> p k n", p=128)
        dn_v = ffn_down.rearrange("(k p) n -> p k n", p=128)
        wo_v = w_o.rearrange("(k p) n -> p k n", p=128)

        for wi in range(nwin):
            col0 = wi * M
            for (t_, b_, src, kk, key) in ((xs, xsb, xT, 8, "xs"), (pa_t, pa_b, partT, 8, "pa"), (gd_t, gd_b, gdT, 8, "gd"), (o_t, o_b, oT, 4, "o")):
                kb.dma("sp", key, lambda e, t_=t_, src=src, kk=kk, col0=col0: e.dma_start(
                    out=t_[:, :].rearrange("p (k n) -> p k n", k=kk), in_=src.rearrange("(k p) n -> p k n", p=128)[:, :, col0:col0 + W]),
                    writes=[b_])
            wd = wload(kb, c, [(diff_out.rearrange("(k p) n -> p k n", p=128), 0, 1024)], 4, 1024)
            for oc in range(8):
                ypt, ypb = proj(kb, wd[0], wd[1], 4, 1024, oc * 128, o_t, o_b, W, 0, W)
                tt, tb_ = kb.get("t32")
                kb.op("dve", lambda e, tt=tt, ypt=ypt, oc=oc: e.tensor_tensor(out=tt[:, 0:W], in0=ypt[:, 0:W], in1=gd_t[:, oc * W:(oc + 1) * W], op=ALU.mult),
                      reads=[ypb, gd_b], writes=[tb_])
                kb.op("pool", lambda e, tt=tt, oc=oc: e.tensor_tensor(out=m_t[:, oc * W:(oc + 1) * W], in0=tt[:, 0:W], in1=pa_t[:, oc * W:(oc + 1) * W], op=ALU.add),
                      reads=[tb_, pa_b], writes=[m_b])
            ww = None
            for oc in range(8):
                if oc % 4 == 0:
                    ww = wload(kb, c, [(wo_v[:, :, 512 * (oc // 4):512 * (oc // 4) + 512], 0, 512)], 8, 512)
                ypt, ypb = proj(kb, ww[0], ww[1], 8, 512, (oc % 4) * 128, m_t, m_b, W, 0, W)
                kb.op("dve", lambda e, ypt=ypt, oc=oc: e.tensor_tensor(out=xm_t[:, oc * W:(oc + 1) * W], in0=ypt[:, 0:W], in1=xs[:, oc * W:(oc + 1) * W], op=ALU.add),
                      reads=[ypb, xsb], writes=[xm_b])
            rmsnorm_h(kb, c, xm_t, xm_b, g_t, g_b, h_t, h_b, sq_t, sq_b)
            wu = None
            for cc in range(22):
                if cc % 2 == 0:
                    j = cc // 2
                    wu = wload(kb, c, [(up_v[:, :, 256 * j:256 * j + 256], 0, 256), (up_v[:, :, D_FF + 256 * j:D_FF + 256 * j + 256], 256, 256)], 8, 512)
                res = []
                for half in range(2):
                    ch = cc + 22 * half
                    ppt, ppb = proj(kb, wu[0], wu[1], 8, 512, half * 256 + (cc % 2) * 128, h_t, h_b, W, 0, W)
                    ct, cb_ = kb.get("t32")
                    kb.op("dve", lambda e, ct=ct, ppt=ppt, ch=ch: e.tensor_scalar(out=ct[:, 0:M], in0=ppt[:, H - 2:W - 2], scalar1=fw_t[:, ch:ch + 1],
                                                                                 scalar2=None, op0=ALU.mult), reads=[ppb, fw_b], writes=[cb_])
                    for tap in (1, 2):
                        kb.op("dve", lambda e, ct=ct, ppt=ppt, ch=ch, tap=tap: e.scalar_tensor_tensor(
                            out=ct[:, 0:M], in0=ppt[:, H - 2 + tap:W - 2 + tap], scalar=fw_t[:, tap * 44 + ch:tap * 44 + ch + 1],
                            in1=ct[:, 0:M], op0=ALU.mult, op1=ALU.add), reads=[ppb, fw_b, cb_], writes=[cb_])
                    res.append((ct, cb_))
                (cg, cgb), (cu, cub) = res
                kb.op("act", lambda e, cg=cg: e.activation(out=cg[:, 0:M], in_=cg[:, 0:M], func=AF.Silu), reads=[cgb], writes=[cgb])
                kb.op("pool", lambda e, cg=cg, cu=cu, cc=cc: e.tensor_tensor(out=f_t[:, cc * M:(cc + 1) * M], in0=cg[:, 0:M], in1=cu[:, 0:M], op=ALU.mult),
                      reads=[cgb, cub], writes=[f_b])
            for oc in range(8):
                wdn = wload(kb, c, [(dn_v[:, :, oc * 128:(oc + 1) * 128], 0, 128)], 22, 128)
                ypt, ypb = proj(kb, wdn[0], wdn[1], 22, 128, 0, f_t, f_b, M, 0, M)
                kb.op("dve", lambda e, ypt=ypt, oc=oc: e.tensor_tensor(out=xo_t[:, oc * M:(oc + 1) * M], in0=ypt[:, 0:M], in1=xm_t[:, oc * W + H:(oc + 1) * W], op=ALU.add),
                      reads=[ypb, xm_b], writes=[xo_b])
            ob = Buf("out")
            kb.dma("sp", "xo", lambda e, col0=col0: e.dma_start(out=xo.rearrange("(k p) n -> p k n", p=128)[:, :, col0:col0 + M],
                                                                in_=xo_t[:, :].rearrange("p (k n) -> p k n", k=8)), reads=[xo_b], writes=[ob])
        eng = kb.engs["sp"]
        waits = kb._deps(eng, [], [xo_b])
        eng.ops.append((waits, None, None, 0))
        kb.emit()
    return nc


QT = 512
NQT = SEQ // QT


def build_l2(lam_init, nqt=NQT):
    nc = bass.Bass("TRN2", target_bir_lowering=False)
    dt = nc.dram_tensor
    qT = dt("qT", [128, SEQ], BF16, kind="ExternalInput").ap()
    kT = dt("kT", [128, SEQ], BF16, kind="ExternalInput").ap()
    vtm = dt("v", [SEQ, 128], BF16, kind="ExternalInput").ap()
    masks = dt("masks", [128, 4 * QT], BF16, kind="ExternalInput").ap()
    lams = dt("lams", [4, 64], F32, kind="ExternalInput").ap()
    subln = dt("subln", [128], F32, kind="ExternalInput").ap()
    oT = dt("oT", [128, SEQ], BF16, kind="ExternalOutput").ap()

    with contextlib.ExitStack() as st:
        kb = KB(nc, st)
        q_t, q_b = kb.sb("q", [128, SEQ], BF16)
        k_t, k_b = kb.sb("k", [128, SEQ], BF16)
        v_t, v_b = kb.sb("v", [128, SEQ], BF16)
        mk_t, mk_b = kb.sb("mk", [128, 4 * QT], BF16)
        ones_t, ones_b = kb.sb("ones", [128, 128], BF16)
        o128_t, o128_b = kb.sb("o128", [128, 128], BF16)
        eps_t, eps_b = kb.sb("eps", [128, 1], F32)
        kb.op("pool", lambda e: e.memset(ones_t[:], 1.0), writes=[ones_b])
        kb.op("pool", lambda e: e.memset(o128_t[:], 1.0 / 128.0), writes=[o128_b])
        kb.op("pool", lambda e: e.memset(eps_t[:], EPS), writes=[eps_b])
        NCH = 8
        for i in range(NCH):
            sl = slice(i * SEQ // NCH, (i + 1) * SEQ // NCH)
            kb.dma("sp", "q", lambda e, sl=sl: e.dma_start(out=q_t[:, sl], in_=qT[:, sl]), writes=[q_b])
            kb.dma("sp", "k", lambda e, sl=sl: e.dma_start(out=k_t[:, sl], in_=kT[:, sl]), writes=[k_b])
        for i in range(32):
            sl = slice(i * SEQ // 32, (i + 1) * SEQ // 32)
            kb.dma("sp", "v", lambda e, sl=sl: e.dma_start(out=v_t[:, sl].rearrange("p (i d) -> p i d", d=128),
                                                           in_=vtm[sl, :].rearrange("(i p) d -> p i d", p=128)), writes=[v_b])
        kb.dma("sp", "mk", lambda e: e.dma_start(out=mk_t[:, :], in_=masks), writes=[mk_b])
        lm_t, lm_b = kb.sb("lm", [128, 4 * 64], F32)
        kb.dma("sp", "lm", lambda e: e.dma_start(out=lm_t[:, :], in_=lams.rearrange("a b -> (a b)").partition_broadcast(128)), writes=[lm_b])
        gs_t, gs_b = kb.sb("gs", [128, 1], F32)
        kb.dma("sp", "gs", lambda e: e.dma_start(out=gs_t[:, :], in_=subln.rearrange("(p o) -> p o", o=1), allow_slow_non_contiguous=True), writes=[gs_b])
        kb.op("dve", lambda e: e.tensor_scalar(out=gs_t[:, :], in0=gs_t[:, :], scalar1=1.0 - lam_init, scalar2=None, op0=ALU.mult), reads=[gs_b], writes=[gs_b])
        pr_t, pr_b = kb.sb("pr", [128, 128], F32)
        sm_t, sm_b = kb.sb("sm", [128, 4], F32)
        for a in range(2):
            kb.op("dve", lambda e, a=a: e.tensor_tensor(out=pr_t[:, a * 64:(a + 1) * 64], in0=lm_t[:, (2 * a) * 64:(2 * a + 1) * 64],
                                                       in1=lm_t[:, (2 * a + 1) * 64:(2 * a + 2) * 64], op=ALU.mult), reads=[lm_b], writes=[pr_b])
            kb.op("dve", lambda e, a=a: e.reduce_sum(out=sm_t[:, a:a + 1], in_=pr_t[:, a * 64:(a + 1) * 64], axis=mybir.AxisListType.X),
                  reads=[pr_b], writes=[sm_b])
        kb.op("act", lambda e: e.activation(out=sm_t[:, 0:2], in_=sm_t[:, 0:2], func=AF.Exp), reads=[sm_b], writes=[sm_b])
        kb.op("dve", lambda e: e.tensor_tensor(out=sm_t[:, 2:3], in0=sm_t[:, 1:2], in1=sm_t[:, 0:1], op=ALU.subtract), reads=[sm_b], writes=[sm_b])
        kb.op("dve", lambda e: e.tensor_scalar(out=sm_t[:, 3:4], in0=sm_t[:, 2:3], scalar1=-lam_init, scalar2=None, op0=ALU.add), reads=[sm_b], writes=[sm_b])

        kb.pool("ps", 4, [128, 512], F32, psum=True)
        accs = [kb.ps("acc%d" % i, [128, 512], F32) for i in range(4)]
        kb.pool("e", 6, [128, QT], BF16)
        kb.pool("t32", 6, [128, QT], F32)
        kb.pool("ob", 2, [128, QT], BF16)

        for j in range(nqt):
            nk = 4 * (j + 1)
            qs = slice(j * QT, (j + 1) * QT)
            for i in range(nk):
                ks = slice(i * 128, (i + 1) * 128)
                es = []
                for hf in range(2):
                    spt, spb = kb.get("ps")
                    kb.op("pe", lambda e, spt=spt, hf=hf, ks=ks, qs=qs: e.matmul(spt[:, :], lhsT=k_t[64 * hf:64 * hf + 64, ks], rhs=q_t[64 * hf:64 * hf + 64, qs],
                                                                        start=True, stop=True), reads=[k_b, q_b], writes=[spb])
                    et, eb = kb.get("e")
                    kb.op("act", lambda e, et=et, spt=spt: e.activation(out=et[:, :], in_=spt[:, :], func=AF.Exp), reads=[spb], writes=[eb])
                    if i >= 4 * j:
                        r = i - 4 * j
                        kb.op("pool" if hf else "dve", lambda e, et=et, r=r: e.tensor_tensor(out=et[:, :], in0=et[:, :], in1=mk_t[:, r * QT:(r + 1) * QT], op=ALU.mult),
                              reads=[eb, mk_b], writes=[eb])
                    es.append((et, eb))
                for hf in range(2):
                    et, eb = es[hf]
                    kb.op("pe", lambda e, et=et, hf=hf, ks=ks, i=i, nk=nk: e.matmul(accs[hf][0][:, :], lhsT=v_t[:, ks], rhs=et[:, :], start=(i == 0), stop=(i == nk - 1)),
                          reads=[v_b, eb], writes=[accs[hf][1]])
                    kb.op("pe", lambda e, et=et, hf=hf, i=i, nk=nk: e.matmul(accs[2 + hf][0][:, :], lhsT=ones_t[:, :], rhs=et[:, :], start=(i == 0), stop=(i == nk - 1)),
                          reads=[ones_b, eb], writes=[accs[2 + hf][1]])
            rs = []
            for hf in range(2):
                rt, rb = kb.get("t32")
                kb.op("dve", lambda e, rt=rt, hf=hf: e.reciprocal(out=rt[:, :], in_=accs[2 + hf][0][:, :]), reads=[accs[2 + hf][1]], writes=[rb])
                kb.op("dve", lambda e, rt=rt, hf=hf: e.tensor_tensor(out=rt[:, :], in0=accs[hf][0][:, :], in1=rt[:, :], op=ALU.mult),
                      reads=[accs[hf][1], rb], writes=[rb])
                rs.append((rt, rb))
            (a_t, a_b), (b_t, b_b) = rs
            kb.op("dve", lambda e, a_t=a_t, b_t=b_t: e.scalar_tensor_tensor(out=a_t[:, :], in0=b_t[:, :], scalar=sm_t[:, 3:4], in1=a_t[:, :],
                                                                          op0=ALU.mult, op1=ALU.add), reads=[b_b, sm_b, a_b], writes=[a_b])
            sq_t, sq_b = kb.get("e")
            kb.op("act", lambda e, sq_t=sq_t, a_t=a_t: e.activation(out=sq_t[:, :], in_=a_t[:, :], func=AF.Square), reads=[a_b], writes=[sq_b])
            spt, spb = kb.get("ps")
            kb.op("pe", lambda e, spt=spt, sq_t=sq_t: e.matmul(spt[:, :], lhsT=o128_t[:, :], rhs=sq_t[:, :], start=True, stop=True),
                  reads=[o128_b, sq_b], writes=[spb])
            r_t, r_b = kb.get("t32")
            kb.op("act", lambda e, r_t=r_t, spt=spt: e.activation(out=r_t[:, :], in_=spt[:, :], func=AF.Ln, bias=eps_t[:, 0:1]), reads=[spb, eps_b], writes=[r_b])
            kb.op("act", lambda e, r_t=r_t: e.activation(out=r_t[:, :], in_=r_t[:, :], func=AF.Exp, scale=-0.5), reads=[r_b], writes=[r_b])
            ot, otb = kb.get("ob")
            kb.op("dve", lambda e, ot=ot, a_t=a_t, r_t=r_t: e.scalar_tensor_tensor(out=ot[:, :], in0=a_t[:, :], scalar=gs_t[:, 0:1], in1=r_t[:, :],
                                                                                op0=ALU.mult, op1=ALU.mult), reads=[a_b, gs_b, r_b], writes=[otb])
            ob = Buf("out")
            kb.dma("sp", "ob%d" % (j % 2), lambda e, ot=ot, qs=qs: e.dma_start(out=oT[:, qs], in_=ot[:, :]), reads=[otb], writes=[ob])
            c_last = otb
        eng = kb.engs["sp"]
        fin = [it[1] for it in kb.pools["ob"][0]]
        waits = kb._deps(eng, [], fin)
        eng.ops.append((waits, None, None, 0))
        kb.emit()
    return nc


def causal_masks():
    m = np.zeros((128, 4 * QT), np.float32)
    p = np.arange(128)[:, None]
    col = np.arange(QT)[None, :]
    for r in range(4):
        m[:, r * QT:(r + 1) * QT] = (col >= 128 * r + p)
    return m.astype(NPBF)


_PROGS = {}


def _prog(name, fn):
    if name not in _PROGS:
        _PROGS[name] = fn()
    return _PROGS[name]


def kernel(**inp):
    inp = {k: np.asarray(v) for k, v in inp.items()}
    cores = list(range(NCORE))
    xT_b = [np.ascontiguousarray(inp["x"][b].T) for b in range(NB)]
    masks = causal_masks()
    for l in range(2):
        lam_init = 0.8 - 0.6 * math.exp(-0.3 * l)
        r1 = run_bass_kernel_spmd(_prog("l1", build_l1), l1_maps(inp, l, xT_b), core_ids=cores).results
        qkv_b, gd_b, part_b = [], [], []
        for b in range(NB):
            qkv_b.append(np.concatenate([np.asarray(r1[b * 4 + c]["qkv"])[:, :TOK] for c in range(4)], axis=1))
            gd_b.append(np.concatenate([np.asarray(r1[b * 4 + c]["gd"])[:, :TOK] for c in range(4)], axis=1))
            part_b.append(np.concatenate([np.asarray(r1[b * 4 + c]["part"])[:, :TOK] for c in range(4)], axis=1))
        del r1
        lams = np.stack([inp["lambda_q1"][l], inp["lambda_k1"][l], inp["lambda_q2"][l], inp["lambda_k2"][l]]).astype(np.float32)
        maps = []
        for core in cores:
            b, h = divmod(core, 4)
            maps.append({"qT": np.ascontiguousarray(qkv_b[b][128 * h:128 * h + 128]),
                         "kT": np.ascontiguousarray(qkv_b[b][512 + 128 * h:512 + 128 * h + 128]),
                         "v": np.ascontiguousarray(qkv_b[b][1024 + 128 * h:1024 + 128 * h + 128].T),
                         "masks": masks, "lams": lams, "subln": np.ascontiguousarray(inp["diff_subln"][l])})
        r2 = run_bass_kernel_spmd(_prog("l2_%d" % l, lambda: build_l2(lam_init)), maps, core_ids=cores).results
        o_b = [np.concatenate([np.asarray(r2[b * 4 + h]["oT"]) for h in range(4)], axis=0) for b in range(NB)]
        del r2, qkv_b
        maps = []
        for core in cores:
            b, c = divmod(core, 4)
            m = {"xT": halo_slice(xT_b[b], c), "partT": halo_slice(part_b[b], c), "gdT": halo_slice(gd_b[b], c), "oT": halo_slice(o_b[b], c)}
            for k in ("diff_out", "w_o", "ffn_norm", "ffn_up", "ffn_conv", "ffn_down"):
                m[k] = np.ascontiguousarray(inp[k][l])
            maps.append(m)
        r3 = run_bass_kernel_spmd(_prog("l3", build_l3), maps, core_ids=cores).results
        xT_b = [np.concatenate([np.asarray(r3[b * 4 + c]["xo"])[:, :TOK] for c in range(4)], axis=1) for b in range(NB)]
        del r3
    out = np.stack([np.ascontiguousarray(xT_b[b].T) for b in range(NB)]).astype(np.float32)
    return out
```
